# Optimizing a Trainium2 kernel written in Bass

```python
import jax, jax.numpy as jnp
from jax import lax
import numpy as np

D_MODEL = 4096
BATCH = 2
SEQ = 8192
DEPTH = 1

CHUNK = 64
MIX_WIDTH = D_MODEL
SB_WIDTH = MIX_WIDTH // 2
SB_HEAD_DIM = 128
SB_HEADS = SB_WIDTH // SB_HEAD_DIM
POOL_WIDTH = MIX_WIDTH - SB_WIDTH
POOL_WINDOWS = (2, 4, 8, 16)
POOL_GROUPS = len(POOL_WINDOWS)
POOL_GROUP_DIM = POOL_WIDTH // POOL_GROUPS
IN_WIDTH = 3 * SB_WIDTH + POOL_WIDTH
D_FF = ((8 * D_MODEL // 3 + 255) // 256) * 256
CONV_WIDTH = 3
QUERY_BLOCK = 128
NORM_EPS = 1e-6

kernel_name = "hybrid_stickbreak_pool_convffn_block"


def rmsnorm(x, gain):
    xf = x.astype(jnp.float32)
    y = xf * lax.rsqrt(jnp.mean(xf * xf, axis=-1, keepdims=True) + NORM_EPS)
    return (y * gain.astype(jnp.float32)).astype(x.dtype)


def stick_breaking_attention(q, k, v):
    seq = q.shape[2]
    scale = SB_HEAD_DIM ** -0.5
    outs = []
    for start in range(0, seq, QUERY_BLOCK):
        end = start + QUERY_BLOCK
        qb = q[:, :, start:end]
        kb = k[:, :, :end]
        vb = v[:, :, :end]
        z = jnp.einsum('bhqd,bhkd->bhqk', qb, kb) * scale
        t_pos = start + jnp.arange(QUERY_BLOCK)[:, None]
        s_pos = jnp.arange(end)[None, :]
        before = s_pos < t_pos
        log_beta = jax.nn.log_sigmoid(z)
        log_keep = jnp.where(before, log_beta - z, 0.0)
        suffix = lax.cumsum(log_keep, axis=3, reverse=True) - log_keep
        weights = jnp.where(before, jnp.exp(log_beta + suffix), 0.0)
        outs.append(jnp.einsum('bhqk,bhkd->bhqd', weights, vb))
    return jnp.concatenate(outs, axis=2)


def multiscale_pool(u, pool_w, pool_scale):
    b, s, _ = u.shape
    uf = u.astype(jnp.float32).reshape(b, s, POOL_GROUPS, POOL_GROUP_DIM)
    cs = jnp.cumsum(uf, axis=1)
    pos = jnp.arange(1, s + 1, dtype=jnp.float32)
    groups = []
    for g, w in enumerate(POOL_WINDOWS):
        c = cs[:, :, g]
        lagged = jnp.pad(c, ((0, 0), (w, 0), (0, 0)))[:, :s]
        count = jnp.minimum(pos, jnp.float32(w))[None, :, None]
        groups.append((c - lagged) / count - uf[:, :, g])
    pooled = jnp.stack(groups, axis=2).astype(u.dtype)
    mixed = jnp.einsum('bsgc,gcd->bsgd', pooled, pool_w)
    return mixed.reshape(b, s, POOL_WIDTH) * pool_scale


def token_mixer(h, w_in, pool_w, pool_scale, w_out):
    b, s, _ = h.shape
    proj = h @ w_in
    q = proj[..., 0 * SB_WIDTH:1 * SB_WIDTH]
    k = proj[..., 1 * SB_WIDTH:2 * SB_WIDTH]
    v = proj[..., 2 * SB_WIDTH:3 * SB_WIDTH]
    u = proj[..., 3 * SB_WIDTH:]

    def heads(t):
        return t.astype(jnp.float32).reshape(b, s, SB_HEADS, SB_HEAD_DIM).transpose(0, 2, 1, 3)

    attn = stick_breaking_attention(heads(q), heads(k), heads(v))
    attn = attn.transpose(0, 2, 1, 3).reshape(b, s, SB_WIDTH).astype(h.dtype)
    pooled = multiscale_pool(u, pool_w, pool_scale).astype(h.dtype)
    return jnp.concatenate([attn, pooled], axis=-1) @ w_out


def conv_gated_mlp(h, w_up, conv_w, conv_b, w_down):
    up = h @ w_up
    s = up.shape[1]
    padded = jnp.pad(up, ((0, 0), (CONV_WIDTH - 1, 0), (0, 0)))
    conv = conv_b + conv_w[0] * padded[:, 0:s]
    for i in range(1, CONV_WIDTH):
        conv = conv + conv_w[i] * padded[:, i:i + s]
    gate, value = jnp.split(conv, 2, axis=-1)
    return (jax.nn.gelu(gate, approximate=True) * value) @ w_down


def setup_inputs(seed: int = 0) -> dict:
    key = jax.random.key(seed)
    ks = jax.random.split(key, 14)
    f32 = jnp.float32

    def normal(k, shape, scale):
        return jax.random.normal(k, shape, f32) * scale

    def gain(k):
        return 1.0 + normal(k, (DEPTH, D_MODEL), 0.02)

    return {
        "x": normal(ks[0], (BATCH, SEQ, D_MODEL), 1.0),
        "pre_mix_norm": gain(ks[1]),
        "w_in": normal(ks[2], (DEPTH, D_MODEL, IN_WIDTH), D_MODEL ** -0.5),
        "pool_w": normal(ks[3], (DEPTH, POOL_GROUPS, POOL_GROUP_DIM, POOL_GROUP_DIM), POOL_GROUP_DIM ** -0.5),
        "pool_scale": 1.0 + normal(ks[4], (DEPTH, POOL_WIDTH), 0.02),
        "w_out": normal(ks[5], (DEPTH, MIX_WIDTH, D_MODEL), MIX_WIDTH ** -0.5),
        "post_mix_norm": gain(ks[6]),
        "pre_ffn_norm": gain(ks[7]),
        "w_up": normal(ks[8], (DEPTH, D_MODEL, 2 * D_FF), D_MODEL ** -0.5),
        "conv_w": normal(ks[9], (DEPTH, CONV_WIDTH, 2 * D_FF), CONV_WIDTH ** -0.5),
        "conv_b": normal(ks[10], (DEPTH, 2 * D_FF), 0.01),
        "w_down": normal(ks[11], (DEPTH, D_FF, D_MODEL), D_FF ** -0.5),
        "post_ffn_norm": gain(ks[12]),
    }


def reference(x, pre_mix_norm, w_in, pool_w, pool_scale, w_out, post_mix_norm,
              pre_ffn_norm, w_up, conv_w, conv_b, w_down, post_ffn_norm):
    for l in range(DEPTH):
        h = rmsnorm(x, pre_mix_norm[l])
        mix = token_mixer(h, w_in[l], pool_w[l], pool_scale[l], w_out[l])
        x = x + rmsnorm(mix, post_mix_norm[l])
        h = rmsnorm(x, pre_ffn_norm[l])
        ffn = conv_gated_mlp(h, w_up[l], conv_w[l], conv_b[l], w_down[l])
        x = x + rmsnorm(ffn, post_ffn_norm[l])
    return x
```

```python
import numpy as np
from contextlib import ExitStack
import concourse.bass as bass
import concourse.mybir as mybir
from concourse.bass_utils import run_bass_kernel_spmd

F32 = mybir.dt.float32
BF16 = mybir.dt.bfloat16
AF = mybir.ActivationFunctionType
ALU = mybir.AluOpType

NCORES = 8
D = 4096
SEQ = 8192
TOWN = 2048
DFF = 11008
NF = DFF // 128
EPS = 1e-6
POOL_W = (2, 4, 8, 16)
GELU_C = 2.0 * 0.7978845608028654


class Res:
    __slots__ = ("name", "w", "r", "dsem", "dcnt")

    def __init__(self, name):
        self.name = name
        self.w = None
        self.r = {}
        self.dsem = None
        self.dcnt = 0


class Sched:
    ENG = ("pe", "act", "dve", "pool", "sp")

    def __init__(self, nc, es):
        self.nc = nc
        self.es = es
        self.prog = {e: [] for e in self.ENG}
        self.sem = {}
        self.cnt = {}
        for e in ("pe", "act", "dve", "pool"):
            self.sem[e] = es.enter_context(nc.semaphore("c_" + e))
            self.cnt[e] = 0
        self.seen = {e: {} for e in self.ENG}
        self.dma_res = []
        self.free_dsems = []
        self.nd = 0

    def _wait(self, eng, t):
        if t is None:
            return
        name, sem, val = t
        if eng == "pe" and name == "c_pe":
            return
        if self.seen[eng].get(name, 0) >= val:
            return
        self.seen[eng][name] = val
        self.prog[eng].append(lambda e, sem=sem, val=val: e.wait_ge(sem, val))

    def deps(self, eng, reads, writes):
        for r in reads:
            self._wait(eng, r.w)
        for w in writes:
            self._wait(eng, w.w)
            for t in w.r.values():
                self._wait(eng, t)

    def _mark(self, t, reads, writes):
        for r in reads:
            r.r[t[0]] = t
        for w in writes:
            w.w = t
            w.r = {}

    def op(self, eng, fn, reads=(), writes=()):
        self.deps(eng, reads, writes)
        self.cnt[eng] += 1
        sem = self.sem[eng]
        t = ("c_" + eng, sem, self.cnt[eng])
        self.prog[eng].append(lambda e, fn=fn, sem=sem: fn(e).then_inc(sem, 1))
        self._mark(t, reads, writes)

    def _dsem(self, res):
        if res.dsem is None:
            if self.free_dsems:
                res.dsem, res.dcnt = self.free_dsems.pop()
            else:
                self.nd += 1
                nm = "d%d" % self.nd
                res.dsem = (nm, self.es.enter_context(self.nc.semaphore(nm)))
                res.dcnt = 0
            self.dma_res.append(res)

    DESC_LIMIT = 1024

    def dma(self, q, out_ap, in_ap, semres, reads=(), writes=(), first=True):
        if first:
            self.deps(q, reads, writes)
        self._dsem(semres)
        nm, sem = semres.dsem
        shp = tuple(out_ap.shape)
        pieces = [(out_ap, in_ap)]
        if len(shp) >= 3:
            nd = 1
            for d_ in shp[:-1]:
                nd *= d_
            if nd > self.DESC_LIMIT:
                inner = nd // shp[1]
                step = max(1, self.DESC_LIMIT // inner)
                pieces = [(out_ap[:, a:min(a + step, shp[1])], in_ap[:, a:min(a + step, shp[1])])
                          for a in range(0, shp[1], step)]
        t = None
        for o_, i_ in pieces:
            semres.dcnt += 16
            t = (nm, sem, semres.dcnt)
            self.prog[q].append(
                lambda e, o=o_, i=i_, sem=sem: e.dma_start(out=o, in_=i).then_inc(sem, 16))
        self._mark(t, reads, writes)
        return t

    def custom(self, q, fn):
        self.prog[q].append(fn)

    def barrier(self):
        ts = []
        for e in ("pe", "act", "dve", "pool"):
            if self.cnt[e] > 0:
                ts.append(("c_" + e, self.sem[e], self.cnt[e]))
        for r in self.dma_res:
            if r.dcnt > 0:
                ts.append((r.dsem[0], r.dsem[1], r.dcnt))
        for e in self.ENG:
            for t in ts:
                if e != "pe" and t[0] == "c_" + e:
                    continue
                self._wait(e, t)
        for r in self.dma_res:
            self.free_dsems.append((r.dsem, r.dcnt))
            r.dsem = None
        self.dma_res = []


class Arena:
    def __init__(self, tensor, nelem):
        self.t = tensor
        self.n = nelem
        self.off = 0

    def reset(self):
        self.off = 0

    def alloc(self, nbytes, dtype, shape=None):
        ne = (nbytes + 63) // 64 * 32
        assert self.off + ne <= self.n, ("arena overflow", self.off, ne, self.n)
        ap = self.t[:, self.off:self.off + nbytes // 2]
        self.off += ne
        if dtype == F32:
            ap = ap.bitcast(F32)
        return ap


class _Stop(Exception):
    pass


def build_program(stop_after=None):
    nc = bass.Bass("TRN2", target_bir_lowering=False)
    es = ExitStack()
    state = {}

    def done(name):
        if stop_after == name:
            raise _Stop()

    def din(name, shape, dt=F32):
        return nc.dram_tensor(name, list(shape), dt, kind="ExternalInput")

    def dscr(name, shape, dt):
        return nc.dram_tensor(name, list(shape), dt)

    xb = din("xb", [SEQ, D])
    xo = din("xo", [128 + TOWN, D])
    wqkv = din("wqkv", [D, 1536])
    wu = din("wu", [D, 2048])
    poolw = din("poolw", [2048, 512])
    wout_s = din("wout_s", [512, D])
    wup_s = din("wup_s", [512, 2 * DFF])
    wdn_s = din("wdn_s", [DFF, 512])
    g_in = {k: din(k, [1, D]) for k in ("g_pre_mix", "g_post_mix", "g_pre_ffn", "g_post_ffn")}
    pscale_d = din("pscale", [128, 16])
    cw_d = din("cw", [128, 3 * 172])
    cb_d = din("cb", [128, 172])
    ident_d = din("ident", [128, 128])
    tri_d = din("tri", [128, 128])
    nones_d = din("nones", [128, 128])
    masks_d = din("masks", [128, 4 * 512])
    ic_d = din("ic", [128, 64])
    sela_d = din("sela", [128, 8])
    selh_d = din("selh", [128, 8])
    out_d = nc.dram_tensor("out", [TOWN, D], F32, kind="ExternalOutput")

    Wq = dscr("Wq", [3 * 128, 32 * 512], BF16)
    Wu = dscr("Wu", [4 * 128, 32 * 512], BF16)
    Wp = dscr("Wp", [128, 16 * 512], BF16)
    S_out = dscr("S_out", [8 * 128, 2048], BF16)
    G_out = dscr("G_out", [8 * 8 * 128, 2048], BF16)
    UPC = [(0, 22), (22, 22), (44, 21), (65, 21)]
    S_up = [dscr("S_up%d" % i, [n * 128, 1024], BF16) for i, (f0, n) in enumerate(UPC)]
    G_up = [dscr("G_up%d" % i, [8 * n * 128, 1024], BF16) for i, (f0, n) in enumerate(UPC)]
    S_dn = [dscr("S_dn%d" % i, [43 * 128, 512], BF16) for i in range(2)]
    G_dn = [dscr("G_dn%d" % i, [8 * 43 * 128, 512], BF16) for i in range(2)]
    Qs = dscr("Qs", [2 * 4 * 128, SEQ], BF16)
    Vs = dscr("Vs", [SEQ, 512], BF16)
    Ms = dscr("Ms", [2048, TOWN], BF16)
    As = dscr("As", [2048, 2048], BF16)
    Ag = dscr("Ag", [8 * 2048, 2048], BF16)
    Ao = dscr("Ao", [2048, TOWN], BF16)
    MIXs = dscr("MIXs", [TOWN, D], F32)
    X1 = dscr("X1", [128 + TOWN, D], F32)
    Xl = dscr("Xl", [2, D], F32)
    Xg = dscr("Xg", [16, D], F32)
    FFs = dscr("FFs", [TOWN, D], F32)

    arena_t = es.enter_context(nc.sbuf_tensor("arena", [128, 94208], BF16))
    A = Arena(arena_t, 94208)
    banks = [es.enter_context(nc.psum_tensor("ps%d" % i, [128, 512], F32)) for i in range(8)]
    S = Sched(nc, es)

    def load_consts():
        c = {}
        stg = A.alloc(4 * 512 * 4, F32)
        stg_r = Res("cstg")

        def ld_cast(name, src, n):
            dst = A.alloc(n * 2, BF16)
            r = Res(name)
            S.dma("sp", stg[:, 0:n], src[:, :], stg_r, writes=[stg_r])
            S.op("dve", lambda e, d=dst, s=stg[:, 0:n]: e.tensor_copy(out=d, in_=s),
                 reads=[stg_r], writes=[r])
            c[name] = (dst, r)
        ld_cast("ident", ident_d, 128)
        ld_cast("tri", tri_d, 128)
        ld_cast("nones", nones_d, 128)
        ld_cast("masks", masks_d, 2048)

        def ld(name, src, n):
            dst = A.alloc(n * 4, F32)
            r = Res(name)
            S.dma("sp", dst, src[:, :], r, writes=[r])
            c[name] = (dst, r)
        ld("pscale", pscale_d, 16)
        ld("cw", cw_d, 3 * 172)
        ld("cb", cb_d, 172)
        ld("ic", ic_d, 64)
        ld("sela", sela_d, 8)
        ld("selh", selh_d, 8)
        for nm in ("ssq", "ssq2"):
            c[nm] = (A.alloc(16 * 8 * 4, F32).rearrange("p (a b) -> p a b", a=16), Res(nm))
        return c

    def load_gain(name):
        g = A.alloc(D * 4, F32)
        r = Res(name)
        S.dma("sp", g, g_in[name].ap().partition_broadcast(128).rearrange("p o d -> p (o d)"), r,
              writes=[r])
        return g, r

    class NormCtx:
        def __init__(self, consts, gain, nslot=2):
            self.ident, self.ident_r = consts["ident"]
            self.gain, self.gain_r = gain
            self.xf = [(A.alloc(D * 4, F32), Res("xf%d" % i)) for i in range(nslot)]
            self.xs = [(A.alloc(D * 2, BF16), Res("xs%d" % i)) for i in range(2)]
            self.ss = [(A.alloc(64, F32), Res("ss%d" % i)) for i in range(2)]
            self.n = 0
            self.pb = 0

        def run(self, src_ap, dst_fn, dst_res, psb):
            i = self.n
            self.n += 1
            xf, xf_r = self.xf[i % len(self.xf)]
            xs, xs_r = self.xs[i % 2]
            ss, ss_r = self.ss[i % 2]
            S.dma("sp", xf, src_ap, xf_r, writes=[xf_r])
            S.op("dve", lambda e: e.memset(ss[:, 0:4], 0.0), writes=[ss_r])
            S.op("act", lambda e: e.activation(out=xs, in_=xf, func=AF.Square, accum_out=ss[:, 0:1]),
                 reads=[xf_r], writes=[xs_r, ss_r])
            S.op("dve", lambda e: e.tensor_scalar(out=ss[:, 1:2], in0=ss[:, 0:1], scalar1=1.0 / D,
                                                  scalar2=EPS, op0=ALU.mult, op1=ALU.add),
                 reads=[ss_r], writes=[ss_r])
            S.op("act", lambda e: e.activation(out=ss[:, 2:3], in_=ss[:, 1:2], func=AF.Sqrt),
                 reads=[ss_r], writes=[ss_r])
            S.op("dve", lambda e: e.reciprocal(out=ss[:, 3:4], in_=ss[:, 2:3]),
                 reads=[ss_r], writes=[ss_r])
            S.op("dve", lambda e: e.scalar_tensor_tensor(out=xs, in0=xf, scalar=ss[:, 3:4], in1=self.gain,
                                                         op0=ALU.mult, op1=ALU.mult),
                 reads=[xf_r, ss_r, self.gain_r], writes=[xs_r])
            for g4 in range(4):
                bi = psb[self.pb % len(psb)]
                self.pb += 1
                pst = banks[bi][:, :].bitcast(BF16)
                ps_r = bank_res[bi]

                def tr(e, g4=g4, pst=pst):
                    ins = None
                    for j in range(8):
                        k = g4 * 8 + j
                        ins = e.transpose(out=pst[:, j * 128:(j + 1) * 128],
                                          in_=xs[:, k * 128:(k + 1) * 128], identity=self.ident)
                    return ins
                S.op("pe", tr, reads=[xs_r, self.ident_r], writes=[ps_r])
                dst = dst_fn(g4 * 8)
                src = pst.rearrange("p (j t) -> p j t", j=8)
                if g4 % 2 == 0:
                    S.op("act", lambda e, d=dst, s=src: e.copy(out=d, in_=s), reads=[ps_r], writes=[dst_res])
                else:
                    S.op("dve", lambda e, d=dst, s=src: e.tensor_copy(out=d, in_=s), reads=[ps_r],
                         writes=[dst_res])

    bank_res = [Res("bank%d" % i) for i in range(8)]

    def mm_group(out_ap, out_res, pairs, reads, start=True, stop=True):
        def fn(e):
            ins = None
            n = len(pairs)
            for i, (l, r) in enumerate(pairs):
                ins = e.matmul(out_ap, l, r, start=(start and i == 0), stop=(stop and i == n - 1))
            return ins
        S.op("pe", fn, reads=reads, writes=[out_res])

    evac_ctr = [0]

    def evac_copy(dst, dst_res, src, src_res, extra_reads=(), eng=None):
        if eng is None:
            eng = "act" if evac_ctr[0] % 2 == 0 else "dve"
            evac_ctr[0] += 1
        if eng == "act":
            S.op("act", lambda e: e.copy(out=dst, in_=src), reads=[src_res, *extra_reads], writes=[dst_res])
        else:
            S.op("dve", lambda e: e.tensor_copy(out=dst, in_=src), reads=[src_res, *extra_reads],
                 writes=[dst_res])

    def body():
        def prep_piece(src_ap, t32_shape_bytes, cast_in_fn, cast_out_fn, dst_ap, store_in_fn, slots, i):
            t32, t32_r, tb, tb_r = slots[i % len(slots)]
            S.dma("sp", cast_in_fn(t32, True), src_ap, t32_r, writes=[t32_r])
            eng = "dve"
            if eng == "dve":
                S.op("dve", lambda e: e.tensor_copy(out=cast_out_fn(tb), in_=cast_in_fn(t32, False)),
                     reads=[t32_r], writes=[tb_r])
            else:
                S.op("act", lambda e: e.copy(out=cast_out_fn(tb), in_=cast_in_fn(t32, False)),
                     reads=[t32_r], writes=[tb_r])
            if isinstance(dst_ap, list):
                for d_, sfn in dst_ap:
                    S.dma("pool", d_, sfn(tb), tb_r, reads=[tb_r])
            else:
                S.dma("pool", dst_ap, store_in_fn(tb), tb_r, reads=[tb_r])

        A.reset()
        P32 = A.alloc(88 * 1024, F32)
        PB = A.alloc(44 * 1024, BF16)
        slots = [(P32, Res("p32"), PB, Res("pb"))]
        pi = 0
        prep_piece(
            wout_s.ap().rearrange("(kk p) x -> p kk x", p=128), None,
            lambda t, ld: t[:, 0:4 * 4096].rearrange("p (kk x) -> p kk x", kk=4) if ld else
            t[:, 0:4 * 4096].rearrange("p (kk j c) -> p j kk c", kk=4, j=8),
            lambda tb: tb[:, 0:8 * 4 * 512].rearrange("p (j kk c) -> p j kk c", j=8, kk=4),
            S_out.ap().rearrange("(j p) x -> p j x", p=128),
            lambda tb: tb[:, 0:8 * 2048].rearrange("p (j x) -> p j x", j=8), slots, pi)
        pi += 1
        for kk in range(4):
            prep_piece(
                wup_s[kk * 128:(kk + 1) * 128, :], None,
                lambda t, ld: t[:, 0:2 * DFF] if ld else
                t[:, 0:2 * DFF].rearrange("p (h f c) -> p f h c", h=2, f=NF),
                lambda tb: tb[:, 0:2 * DFF].rearrange("p (f h c) -> p f h c", f=NF, h=2),
                [(S_up[ci].ap().rearrange("(f p) (kk x) -> p f kk x", p=128, kk=4)[:, :, kk, :],
                  (lambda tb, f0=f0, n=n: tb[:, 0:2 * DFF].rearrange("p (f x) -> p f x", f=NF)[:, f0:f0 + n, :]))
                 for ci, (f0, n) in enumerate(UPC)],
                None, slots, pi)
            pi += 1
        for h2 in range(2):
            f0 = h2 * 43
            prep_piece(
                wdn_s[f0 * 128:(f0 + 43) * 128, :].rearrange("(f p) c -> p f c", p=128), None,
                lambda t, ld: t[:, 0:43 * 512].rearrange("p (f c) -> p f c", f=43),
                lambda tb: tb[:, 0:43 * 512].rearrange("p (f c) -> p f c", f=43),
                S_dn[h2].ap().rearrange("(f p) c -> p f c", p=128),
                lambda tb: tb[:, 0:43 * 512].rearrange("p (f c) -> p f c", f=43), slots, pi)
            pi += 1
        for k0 in range(0, 32, 8):
            prep_piece(
                wqkv[k0 * 128:(k0 + 8) * 128, :].rearrange("(k p) x -> p k x", p=128), None,
                lambda t, ld: t[:, 0:8 * 1536].rearrange("p (k x) -> p k x", k=8) if ld else
                t[:, 0:8 * 1536].rearrange("p (k s c) -> p s k c", k=8, s=3),
                lambda tb: tb[:, 0:3 * 8 * 512].rearrange("p (s k c) -> p s k c", s=3, k=8),
                Wq.ap().rearrange("(s p) (k c) -> p s k c", p=128, k=32)[:, :, k0:k0 + 8, :],
                lambda tb: tb[:, 0:3 * 8 * 512].rearrange("p (s k c) -> p s k c", s=3, k=8), slots, pi)
            pi += 1
        for k0 in range(0, 32, 8):
            prep_piece(
                wu[k0 * 128:(k0 + 8) * 128, :].rearrange("(k p) x -> p k x", p=128), None,
                lambda t, ld: t[:, 0:8 * 2048].rearrange("p (k x) -> p k x", k=8) if ld else
                t[:, 0:8 * 2048].rearrange("p (k s c) -> p s k c", k=8, s=4),
                lambda tb: tb[:, 0:4 * 8 * 512].rearrange("p (s k c) -> p s k c", s=4, k=8),
                Wu.ap().rearrange("(s p) (k c) -> p s k c", p=128, k=32)[:, :, k0:k0 + 8, :],
                lambda tb: tb[:, 0:4 * 8 * 512].rearrange("p (s k c) -> p s k c", s=4, k=8), slots, pi)
            pi += 1
        prep_piece(
            poolw.ap().rearrange("(gk p) c -> p gk c", p=128), None,
            lambda t, ld: t[:, 0:16 * 512].rearrange("p (gk c) -> p gk c", gk=16),
            lambda tb: tb[:, 0:16 * 512].rearrange("p (gk c) -> p gk c", gk=16),
            Wp.ap().rearrange("p (gk c) -> p gk c", gk=16),
            lambda tb: tb[:, 0:16 * 512].rearrange("p (gk c) -> p gk c", gk=16), slots, pi)
        pi += 1
        S.barrier()

        cc_sem = es.enter_context(nc.semaphore("cc"))
        cc_n = [0]

        def allgather(src, dst):
            cc_n[0] += 1
            n = cc_n[0]
            state["cc"] = ("cc", cc_sem, n)
            S.custom("pool", lambda e, s=src, d=dst: e.collective_compute(
                "AllGather", ALU.bypass, replica_groups=[list(range(NCORES))],
                ins=[s.ap().opt()], outs=[d.ap().opt()]).then_inc(cc_sem, 1))
            t = ("cc", cc_sem, n)
            S._wait("pool", t)
            return t

        t_gout = allgather(S_out, G_out)
        for ci in range(4):
            t_gup = allgather(S_up[ci], G_up[ci])
        for ci in range(2):
            t_gdn = allgather(S_dn[ci], G_dn[ci])

        def wait_all(t):
            for e in S.ENG:
                S._wait(e, t)

        done("P")
        A.reset()
        consts = load_consts()
        const_mark = A.off
        gain = load_gain("g_pre_mix")
        nctx = NormCtx(consts, gain)
        hT = A.alloc(32 * 512 * 2, BF16).rearrange("p (k t) -> p k t", k=32)
        hT_r = Res("hT")
        wt = [(A.alloc(32 * 512 * 2, BF16).rearrange("p (k c) -> p k c", k=32), Res("wt%d" % i)) for i in range(2)]
        stg = [(A.alloc(512 * 2, BF16), Res("stg%d" % i)) for i in range(4)]
        wq_v = Wq.ap().rearrange("(s p) (k c) -> s p k c", p=128, k=32)
        nw = 0
        ns = 0
        scale = 128.0 ** -0.5
        for i in range(SEQ // 512):
            for sub in range(4):
                r0 = i * 512 + sub * 128
                nctx.run(xb[r0:r0 + 128, :], lambda k0, sub=sub: hT[:, k0:k0 + 8, sub * 128:(sub + 1) * 128],
                         hT_r, [0, 1])
            for s in range(3):
                w, w_r = wt[nw % 2]
                nw += 1
                S.dma("sp", w, wq_v[s], w_r, writes=[w_r])
                for ch in range(4):
                    bi = 2 + (ns % 4)
                    ps = banks[bi][:, :]
                    if s < 2:
                        mm_group(ps, bank_res[bi],
                                 [(w[:, k, ch * 128:(ch + 1) * 128], hT[:, k, :]) for k in range(32)],
                                 [w_r, hT_r])
                    else:
                        mm_group(ps, bank_res[bi],
                                 [(hT[:, k, ch * 128:(ch + 1) * 128], w[:, k, :]) for k in range(32)],
                                 [w_r, hT_r])
                    st, st_r = stg[ns % 4]
                    ns += 1
                    if s == 0:
                        S.op("act", lambda e, st=st, ps=ps: e.activation(out=st, in_=ps, func=AF.Copy, scale=scale),
                             reads=[bank_res[bi]], writes=[st_r])
                    else:
                        evac_copy(st, st_r, ps, bank_res[bi])
                    if s < 2:
                        dst = Qs[(s * 4 + ch) * 128:(s * 4 + ch + 1) * 128, i * 512:(i + 1) * 512]
                    else:
                        r0 = i * 512 + ch * 128
                        dst = Vs[r0:r0 + 128, :]
                    S.dma("pool", dst, st, st_r, reads=[st_r])
        S.barrier()

        done("A")
        A.off = const_mark
        gain = load_gain("g_pre_mix")
        nctx = NormCtx(consts, gain, nslot=1)
        hT = A.alloc(32 * 512 * 2, BF16).rearrange("p (k t) -> p k t", k=32)
        hT_r = Res("hT")
        wt1 = (A.alloc(32 * 512 * 2, BF16).rearrange("p (k c) -> p k c", k=32), Res("wu"))
        wp = (A.alloc(16 * 512 * 2, BF16).rearrange("p (gk c) -> p gk c", gk=16), Res("wp"))
        S.dma("sp", wp[0], Wp.ap().rearrange("p (gk c) -> p gk c", gk=16), wp[1], writes=[wp[1]])
        uT = A.alloc(4 * 528 * 4, F32).rearrange("p (c t) -> p c t", c=4)
        uT_r = Res("uT")
        uH = A.alloc(16 * 16 * 4, F32).rearrange("p (c t) -> p c t", c=16)
        uH_r = Res("uH")
        tA = A.alloc(4 * 528 * 4, F32).rearrange("p (c t) -> p c t", c=4)
        tB = A.alloc(4 * 528 * 4, F32).rearrange("p (c t) -> p c t", c=4)
        tA_r, tB_r = Res("tA"), Res("tB")
        pl = A.alloc(4 * 512 * 2, BF16).rearrange("p (c t) -> p c t", c=4)
        pl_r = Res("pl")
        fx = A.alloc(4 * 16 * 4, F32).rearrange("p (c t) -> p c t", c=4)
        fx_r = Res("fx")
        stg = [(A.alloc(512 * 2, BF16), Res("stg%d" % i)) for i in range(4)]
        pscale, pscale_r = consts["pscale"]
        ic, ic_r = consts["ic"]
        wu_v = Wu.ap().rearrange("(s p) (k c) -> s p k c", p=128, k=32)
        ns = 0
        for i in range(-1, TOWN // 512):
            if i < 0:
                nctx.run(xo[0:128, :], lambda k0: hT[:, k0:k0 + 8, 0:128], hT_r, [0, 1])
                ntok = 128
            else:
                for sub in range(4):
                    r0 = 128 + i * 512 + sub * 128
                    nctx.run(xo[r0:r0 + 128, :], lambda k0, sub=sub: hT[:, k0:k0 + 8, sub * 128:(sub + 1) * 128],
                             hT_r, [0, 1])
                ntok = 512
            for g in range(4):
                w, w_r = wt1
                S.dma("sp", w, wu_v[g], w_r, writes=[w_r])
                if i >= 0:
                    S.op("dve", lambda e, g=g: e.tensor_copy(out=uT[:, :, 0:16], in_=uH[:, 4 * g:4 * g + 4, :]),
                         reads=[uH_r], writes=[uT_r])
                for ch in range(4):
                    bi = 2 + (ns % 4)
                    ns += 1
                    ps = banks[bi][:, 0:ntok]
                    mm_group(ps, bank_res[bi],
                             [(w[:, k, ch * 128:(ch + 1) * 128], hT[:, k, 0:ntok]) for k in range(32)],
                             [w_r, hT_r])
                    if i < 0:
                        evac_copy(uH[:, 4 * g + ch, :], uH_r, banks[bi][:, 112:128], bank_res[bi])
                    else:
                        evac_copy(uT[:, ch, 16:528], uT_r, ps, bank_res[bi])
                if i < 0:
                    continue
                wdw = POOL_W[g]
                cur, cur_r = uT, uT_r
                tmps = [(tA, tA_r), (tB, tB_r)]
                sh = 1
                ti = 0
                lo = 0
                while sh < wdw:
                    dst, dst_r = tmps[ti % 2]
                    ti += 1
                    nlo = lo + sh
                    S.op("dve", lambda e, dst=dst, cur=cur, nlo=nlo, sh=sh: e.tensor_tensor(
                        out=dst[:, :, nlo:528], in0=cur[:, :, nlo:528], in1=cur[:, :, nlo - sh:528 - sh], op=ALU.add),
                        reads=[cur_r], writes=[dst_r])
                    cur, cur_r = dst, dst_r
                    lo = nlo
                    sh *= 2
                for ch in range(4):
                    S.op("dve", lambda e, ch=ch, cur=cur, wd=wdw: e.scalar_tensor_tensor(
                        out=pl[:, ch, :], in0=cur[:, ch, 16:528], scalar=1.0 / wd, in1=uT[:, ch, 16:528],
                        op0=ALU.mult, op1=ALU.subtract), reads=[cur_r, uT_r], writes=[pl_r])
                if i == 0:
                    for ch in range(4):
                        S.op("dve", lambda e, ch=ch, cur=cur, g=g: e.tensor_tensor(
                            out=fx[:, ch, :], in0=cur[:, ch, 16:32], in1=ic[:, 16 * g:16 * g + 16], op=ALU.mult),
                            reads=[cur_r, ic_r], writes=[fx_r])
                    S.op("dve", lambda e: e.tensor_tensor(out=pl[:, :, 0:16], in0=fx, in1=uT[:, :, 16:32],
                                                          op=ALU.subtract),
                         reads=[fx_r, uT_r], writes=[pl_r])
                S.op("dve", lambda e, g=g: e.tensor_copy(out=uH[:, 4 * g:4 * g + 4, :], in_=uT[:, :, 512:528]),
                     reads=[uT_r], writes=[uH_r])
                for co in range(4):
                    bi = 6 + (co % 2)
                    ps = banks[bi][:, :]
                    mm_group(ps, bank_res[bi],
                             [(wp[0][:, 4 * g + ci, co * 128:(co + 1) * 128], pl[:, ci, :]) for ci in range(4)],
                             [wp[1], pl_r])
                    st, st_r = stg[co]
                    S.op("dve", lambda e, st=st, ps=ps, cc=4 * g + co: e.tensor_scalar(
                        out=st, in0=ps, scalar1=pscale[:, cc:cc + 1], scalar2=None, op0=ALU.mult),
                        reads=[bank_res[bi], pscale_r], writes=[st_r])
                    cc = 4 * g + co
                    S.dma("pool", Ms[cc * 128:(cc + 1) * 128, i * 512:(i + 1) * 512], st, st_r, reads=[st_r])
        S.barrier()

        done("A2")
        A.off = const_mark
        ident, ident_r = consts["ident"]
        tri, tri_r = consts["tri"]
        nones, nones_r = consts["nones"]
        masks, masks_r = consts["masks"]
        qT = [(A.alloc(SEQ * 2, BF16), Res("qT%d" % i)) for i in range(2)]
        kT = [(A.alloc(SEQ * 2, BF16), Res("kT%d" % i)) for i in range(2)]
        vv = [(A.alloc(64 * 128 * 2, BF16).rearrange("p (b d) -> p b d", b=64), Res("v%d" % i)) for i in range(2)]
        Eb = [(A.alloc(512 * 4, F32), Res("E%d" % i)) for i in range(2)]
        SPb = [(A.alloc(512 * 2, BF16), Res("SP%d" % i)) for i in range(3)]
        Wtb = [(A.alloc(512 * 2, BF16), Res("Wt%d" % i)) for i in range(3)]
        S32 = (A.alloc(512 * 4, F32), Res("S32"))
        Sbb = [(A.alloc(512 * 2, BF16), Res("Sb%d" % i)) for i in range(2)]
        ost = [(A.alloc(512 * 2, BF16), Res("ost%d" % i)) for i in range(2)]
        nblk = 0
        nq = 0
        import os as _os
        for hl in range(int(_os.environ.get('KDBG_HEADS', '4'))):
            q, q_r = qT[hl % 2]
            k_, k_r = kT[hl % 2]
            v, v_r = vv[hl % 2]
            S.dma("sp", q, Qs[hl * 128:(hl + 1) * 128, :], q_r, writes=[q_r])
            S.dma("sp", k_, Qs[(4 + hl) * 128:(5 + hl) * 128, :], k_r, writes=[k_r])
            S.dma("sp", v, Vs.ap().rearrange("(b p) c -> p b c", p=128)[:, :, hl * 128:(hl + 1) * 128], v_r,
                  writes=[v_r])
            for i in range(int(_os.environ.get('KDBG_QT', '16'))):
                qs = q[:, i * 512:(i + 1) * 512]
                bo = 6 + (nq % 2)
                pso = banks[bo][:, :]
                nkb = 4 * i + 4
                for n, kb in enumerate(range(nkb - 1, -1, -1)):
                    diag = kb >= 4 * i
                    ks = k_[:, kb * 128:(kb + 1) * 128]
                    bz = nblk % 3
                    bb = 3 + (nblk % 3)
                    E, E_r = Eb[nblk % 2]
                    SP, SP_r = SPb[nblk % 3]
                    Wt, Wt_r = Wtb[nblk % 3]
                    nblk += 1
                    psz = banks[bz][:, :]
                    psb = banks[bb][:, :]
                    mm_group(psz, bank_res[bz], [(ks, qs)], [k_r, q_r])
                    S.op("act", lambda e, E=E, psz=psz: e.activation(out=E, in_=psz, func=AF.Exp),
                         reads=[bank_res[bz]], writes=[E_r])
                    S.op("act", lambda e, E=E, SP=SP: e.activation(out=SP, in_=E, func=AF.Ln, bias=1.0),
                         reads=[E_r], writes=[SP_r])
                    if diag:
                        m = masks[:, (kb - 4 * i) * 512:(kb - 4 * i + 1) * 512]
                        S.op("dve", lambda e, SP=SP, m=m: e.tensor_tensor(out=SP, in0=SP, in1=m, op=ALU.mult),
                             reads=[SP_r, masks_r], writes=[SP_r])
                    pairs = [(ks, qs), (tri, SP)]
                    rds = [k_r, q_r, tri_r, SP_r]
                    if n > 0:
                        Sb, Sb_r = Sbb[(n - 1) % 2]
                        pairs.append((nones, Sb))
                        rds += [nones_r, Sb_r]
                    mm_group(psb, bank_res[bb], pairs, rds)
                    S.op("act", lambda e, Wt=Wt, psb=psb: e.activation(out=Wt, in_=psb, func=AF.Exp),
                         reads=[bank_res[bb]], writes=[Wt_r])
                    if diag:
                        S.op("dve", lambda e, Wt=Wt, m=m: e.tensor_tensor(out=Wt, in0=Wt, in1=m, op=ALU.mult),
                             reads=[Wt_r, masks_r], writes=[Wt_r])
                    mm_group(pso, bank_res[bo], [(v[:, kb, :], Wt)], [v_r, Wt_r],
                             start=(n == 0), stop=(n == nkb - 1))
                    if n < nkb - 1:
                        Sb, Sb_r = Sbb[n % 2]
                        if n == 0:
                            S.op("dve", lambda e, SP=SP: e.tensor_copy(out=S32[0], in_=SP),
                                 reads=[SP_r], writes=[S32[1]])
                            S.op("dve", lambda e, SP=SP, Sb=Sb: e.tensor_copy(out=Sb, in_=SP),
                                 reads=[SP_r], writes=[Sb_r])
                        else:
                            S.op("dve", lambda e, SP=SP: e.tensor_tensor(out=S32[0], in0=S32[0], in1=SP, op=ALU.add),
                                 reads=[SP_r, S32[1]], writes=[S32[1]])
                            S.op("dve", lambda e, Sb=Sb: e.tensor_copy(out=Sb, in_=S32[0]),
                                 reads=[S32[1]], writes=[Sb_r])
                st, st_r = ost[nq % 2]
                nq += 1
                evac_copy(st, st_r, pso, bank_res[bo])
                j = i // 4
                r0 = (j * 4 + hl) * 128
                S.dma("pool", As[r0:r0 + 128, (i % 4) * 512:(i % 4 + 1) * 512], st, st_r, reads=[st_r])
        S.barrier()
        t_ag = allgather(As, Ag)
        wait_all(t_ag)

        done("B")
        A.off = const_mark
        sela, sela_r = consts["sela"]
        cand = [(A.alloc(16 * 512 * 2, BF16).rearrange("p (k t) -> p k t", k=16), Res("cand%d" % i)) for i in range(2)]
        acc = [(A.alloc(16 * 512 * 2, BF16).rearrange("p (k t) -> p k t", k=16), Res("acc%d" % i)) for i in range(2)]
        ag_v = Ag.ap().rearrange("(r j hl d) t -> r j d hl t", r=8, j=4, hl=4)
        ao_v = Ao.ap().rearrange("(k d) t -> d k t", d=128)
        nc_ = 0
        for i in range(4):
            ac, ac_r = acc[i % 2]
            for cd in range(8):
                bq, j = cd // 4, cd % 4
                cn, cn_r = cand[nc_ % 2]
                nc_ += 1
                for pp in range(4):
                    S.dma("sp", cn[:, 4 * pp:4 * pp + 4, :], ag_v[4 * bq + pp, j, :, :, i * 512:(i + 1) * 512],
                          cn_r, writes=[cn_r], first=(pp == 0))
                cf = cn.rearrange("p k t -> p (k t)")
                af = ac.rearrange("p k t -> p (k t)")
                if cd == 0:
                    S.op("dve", lambda e, cf=cf, af=af, cd=cd: e.tensor_scalar(
                        out=af, in0=cf, scalar1=sela[:, cd:cd + 1], scalar2=None, op0=ALU.mult),
                        reads=[cn_r, sela_r], writes=[ac_r])
                else:
                    S.op("dve", lambda e, cf=cf, af=af, cd=cd: e.scalar_tensor_tensor(
                        out=af, in0=cf, scalar=sela[:, cd:cd + 1], in1=af, op0=ALU.mult, op1=ALU.add),
                        reads=[cn_r, sela_r, ac_r], writes=[ac_r])
            S.dma("pool", ao_v[:, :, i * 512:(i + 1) * 512], ac, ac_r, reads=[ac_r])
        S.barrier()

        done("C0")
        wait_all(t_gout)
        A.off = const_mark
        cT = A.alloc(32 * TOWN * 2, BF16).rearrange("p (k t) -> p k t", k=32)
        cT_r = Res("cT")
        S.dma("sp", cT[:, 0:16, :], ao_v, cT_r, writes=[cT_r])
        S.dma("sp", cT[:, 16:32, :], Ms.ap().rearrange("(k d) t -> d k t", d=128), cT_r, writes=[cT_r], first=False)
        wo = (A.alloc(32 * 512 * 2, BF16).rearrange("p (k c) -> p k c", k=32), Res("wo"))
        fst = [(A.alloc(512 * 4, F32), Res("fst%d" % i)) for i in range(2)]
        jk = [(A.alloc(512 * 2, BF16), Res("jk%d" % i)) for i in range(2)]
        ssq = consts["ssq"]
        S.op("dve", lambda e: e.memset(ssq[0].rearrange("p a b -> p (a b)"), 0.0), writes=[ssq[1]])
        gout_v = G_out.ap().rearrange("(r j p) x -> j p r x", r=8, j=8)
        ne = 0
        for j in range(8):
            S.dma("sp", wo[0].rearrange("p (r kk) c -> p r (kk c)", r=8), gout_v[j], wo[1], writes=[wo[1]])
            for tb_ in range(16):
                bi = ne % 8
                ps = banks[bi][:, :]
                mm_group(ps, bank_res[bi],
                         [(cT[:, k, tb_ * 128:(tb_ + 1) * 128], wo[0][:, k, :]) for k in range(32)],
                         [cT_r, wo[1]])
                st, st_r = fst[ne % 2]
                jj, jj_r = jk[ne % 2]
                ne += 1
                S.op("act", lambda e, st=st, ps=ps: e.copy(out=st, in_=ps), reads=[bank_res[bi]],
                     writes=[st_r])
                S.op("act", lambda e, jj=jj, st=st, tb_=tb_, j=j: e.activation(
                    out=jj, in_=st, func=AF.Square, accum_out=ssq[0][:, tb_, j:j + 1]),
                    reads=[st_r], writes=[jj_r, ssq[1]])
                S.dma("pool", MIXs[tb_ * 128:(tb_ + 1) * 128, j * 512:(j + 1) * 512], st, st_r, reads=[st_r])
        S.barrier()

        done("C1")
        def resid_phase(ssq_ap, ssq_res, gname, src_d, res_d, res_row0, dst_d, dst_row0, keep_from):
            A.off = keep_from
            g, g_r = load_gain(gname)
            mx = [(A.alloc(D * 4, F32), Res("mx%d" % i)) for i in range(2)]
            xr = [(A.alloc(D * 4, F32), Res("xr%d" % i)) for i in range(2)]
            rs = (A.alloc(16 * 4 * 4, F32).rearrange("p (a b) -> p a b", a=16), Res("rs"))
            S.op("dve", lambda e: e.tensor_reduce(out=rs[0][:, :, 0], in_=ssq_ap, axis=mybir.AxisListType.X,
                                                  op=ALU.add), reads=[ssq_res], writes=[rs[1]])
            S.op("dve", lambda e: e.tensor_scalar(out=rs[0][:, :, 1], in0=rs[0][:, :, 0], scalar1=1.0 / D,
                                                  scalar2=EPS, op0=ALU.mult, op1=ALU.add),
                 reads=[rs[1]], writes=[rs[1]])
            S.op("act", lambda e: e.activation(out=rs[0][:, :, 2], in_=rs[0][:, :, 1], func=AF.Sqrt),
                 reads=[rs[1]], writes=[rs[1]])
            S.op("dve", lambda e: e.reciprocal(out=rs[0][:, :, 3], in_=rs[0][:, :, 2]), reads=[rs[1]],
                 writes=[rs[1]])
            for tb_ in range(16):
                m, m_r = mx[tb_ % 2]
                x_, x_r = xr[tb_ % 2]
                S.dma("sp", m, src_d[tb_ * 128:(tb_ + 1) * 128, :], m_r, writes=[m_r])
                S.dma("sp", x_, res_d[res_row0 + tb_ * 128:res_row0 + (tb_ + 1) * 128, :], x_r, writes=[x_r])
                S.op("dve", lambda e, m=m, tb_=tb_: e.scalar_tensor_tensor(
                    out=m, in0=m, scalar=rs[0][:, tb_, 3:4], in1=g, op0=ALU.mult, op1=ALU.mult),
                    reads=[m_r, rs[1], g_r], writes=[m_r])
                S.op("dve", lambda e, m=m, x_=x_: e.tensor_tensor(out=x_, in0=x_, in1=m, op=ALU.add),
                     reads=[m_r, x_r], writes=[x_r])
                S.dma("pool", dst_d[dst_row0 + tb_ * 128:dst_row0 + (tb_ + 1) * 128, :], x_, x_r, reads=[x_r])

        resid_phase(ssq[0], ssq[1], "g_post_mix", MIXs, xo, 128, X1, 128, const_mark)
        S.barrier()

        done("C2")
        A.off = const_mark
        selh, selh_r = consts["selh"]
        S.dma("sp", Xl[:, :], X1[128 + TOWN - 2:128 + TOWN, :], Res("xl"))
        S.barrier()
        t_xg = allgather(Xl, Xg)
        wait_all(t_xg)
        hc = [(A.alloc(D * 4, F32), Res("hc%d" % i)) for i in range(2)]
        hacc = (A.alloc(D * 4, F32), Res("hacc"))
        zt = (A.alloc(D * 4, F32), Res("zt"))
        S.op("dve", lambda e: e.memset(zt[0], 0.0), writes=[zt[1]])
        S.dma("pool", X1[0:126, :], zt[0][0:126, :], zt[1], reads=[zt[1]])
        for r in range(8):
            h_, h_r = hc[r % 2]
            S.dma("sp", h_[0:2, :], Xg[2 * r:2 * r + 2, :], h_r, writes=[h_r])
            if r == 0:
                S.op("dve", lambda e, h_=h_, r=r: e.tensor_scalar(
                    out=hacc[0][0:2, :], in0=h_[0:2, :], scalar1=selh[0:2, r:r + 1], scalar2=None, op0=ALU.mult),
                    reads=[h_r, selh_r], writes=[hacc[1]])
            else:
                S.op("dve", lambda e, h_=h_, r=r: e.scalar_tensor_tensor(
                    out=hacc[0][0:2, :], in0=h_[0:2, :], scalar=selh[0:2, r:r + 1], in1=hacc[0][0:2, :],
                    op0=ALU.mult, op1=ALU.add), reads=[h_r, selh_r, hacc[1]], writes=[hacc[1]])
        S.dma("pool", X1[126:128, :], hacc[0][0:2, :], hacc[1], reads=[hacc[1]])
        S.barrier()

        done("H")
        wait_all(t_gup)
        wait_all(t_gdn)
        cw, cw_r = consts["cw"]
        cb, cb_r = consts["cb"]
        gup_c = [G_up[ci].ap().rearrange("(r f p) x -> f p r x", r=8, f=n) for ci, (f0, n) in enumerate(UPC)]

        def gup_v(f):
            for ci, (f0, n) in enumerate(UPC):
                if f0 <= f < f0 + n:
                    return gup_c[ci][f - f0]
        gdn_c = [G_dn[ci].ap().rearrange("(j f p) c -> j p f c", j=8, f=43) for ci in range(2)]
        pieces = [(0, 11), (11, 11), (22, 11), (33, 10), (43, 11), (54, 11), (65, 11), (76, 10)]
        for i in range(TOWN // 512):
            A.off = const_mark
            hT = A.alloc(32 * 512 * 2, BF16).rearrange("p (k t) -> p k t", k=32)
            hT_r = Res("hT")
            hH2 = A.alloc(32 * 16 * 2, BF16).rearrange("p (k t) -> p k t", k=32)
            hH2_r = Res("hH2")
            ACTs = A.alloc(NF * 512 * 2, BF16).rearrange("p (f t) -> p f t", f=NF)
            ACTs_r = Res("ACTs")
            d1_mark = A.off
            A.off = d1_mark - NF * 512
            gain = load_gain("g_pre_ffn")
            nctx = NormCtx(consts, gain, nslot=1)
            hH = A.alloc(32 * 128 * 2, BF16).rearrange("p (k t) -> p k t", k=32)
            hH_r = Res("hH")
            assert A.off <= d1_mark
            r0 = i * 512
            nctx.run(X1[r0:r0 + 128, :], lambda k0: hH[:, k0:k0 + 8, :], hH_r, [0, 1])
            S.op("dve", lambda e: e.tensor_copy(out=hH2, in_=hH[:, :, 112:128]), reads=[hH_r], writes=[hH2_r])
            for sub in range(4):
                r0 = 128 + i * 512 + sub * 128
                nctx.run(X1[r0:r0 + 128, :], lambda k0, sub=sub: hT[:, k0:k0 + 8, sub * 128:(sub + 1) * 128],
                         hT_r, [0, 1])
            S.barrier()
            A.off = d1_mark
            wup = [(A.alloc(8 * 1024 * 2, BF16).rearrange("p (r kk h c) -> p r kk h c", r=8, kk=4, h=2),
                    Res("wup%d" % s)) for s in range(2)]
            GE = [(A.alloc(514 * 4, F32), Res("GE%d" % s)) for s in range(1)]
            VE = [(A.alloc(514 * 4, F32), Res("VE%d" % s)) for s in range(1)]
            CG = [(A.alloc(512 * 4, F32), Res("CG%d" % s)) for s in range(1)]
            CV = [(A.alloc(512 * 4, F32), Res("CV%d" % s)) for s in range(1)]
            T1 = [(A.alloc(512 * 4, F32), Res("T1%d" % s)) for s in range(1)]
            T2 = [(A.alloc(512 * 4, F32), Res("T2%d" % s)) for s in range(1)]
            for f in range(NF):
                w, w_r = wup[f % 2]
                S.dma("sp", w.rearrange("p r kk h c -> p r (kk h c)"), gup_v(f), w_r, writes=[w_r])
                sl = 0
                ge, ge_r = GE[sl]
                ve, ve_r = VE[sl]
                cg, cg_r = CG[sl]
                cv, cv_r = CV[sl]
                t1, t1_r = T1[sl]
                t2, t2_r = T2[sl]
                for h, (xe, xe_r, cx, cx_r) in enumerate(((ge, ge_r, cg, cg_r), (ve, ve_r, cv, cv_r))):
                    bi = (2 * f + h) % 4 + 2
                    bh = 6 + (2 * f + h) % 2
                    ps = banks[bi][:, :]
                    psh = banks[bh][:, 0:16]
                    ch = h * NF + f
                    mm_group(ps, bank_res[bi],
                             [(w[:, k // 4, k % 4, h, :], hT[:, k, :]) for k in range(32)], [w_r, hT_r])
                    mm_group(psh, bank_res[bh],
                             [(w[:, k // 4, k % 4, h, :], hH2[:, k, :]) for k in range(32)], [w_r, hH2_r])
                    S.op("act", lambda e, cx=cx, ps=ps, ch=ch: e.activation(
                        out=cx, in_=ps, func=AF.Identity, scale=cw[:, 2 * 172 + ch:2 * 172 + ch + 1],
                        bias=cb[:, ch:ch + 1]), reads=[bank_res[bi], cw_r, cb_r], writes=[cx_r])
                    S.op("act", lambda e, xe=xe, ps=ps: e.copy(out=xe[:, 2:514], in_=ps),
                         reads=[bank_res[bi]], writes=[xe_r])
                    S.op("dve", lambda e, xe=xe, psh=psh: e.tensor_copy(out=xe[:, 0:2], in_=psh[:, 14:16]),
                         reads=[bank_res[bh]], writes=[xe_r])
                    S.op("dve", lambda e, xe=xe, cx=cx, ch=ch: e.scalar_tensor_tensor(
                        out=cx, in0=xe[:, 1:513], scalar=cw[:, 172 + ch:172 + ch + 1], in1=cx,
                        op0=ALU.mult, op1=ALU.add), reads=[xe_r, cx_r, cw_r], writes=[cx_r])
                    S.op("dve", lambda e, xe=xe, cx=cx, ch=ch: e.scalar_tensor_tensor(
                        out=cx, in0=xe[:, 0:512], scalar=cw[:, ch:ch + 1], in1=cx,
                        op0=ALU.mult, op1=ALU.add), reads=[xe_r, cx_r, cw_r], writes=[cx_r])
                S.op("dve", lambda e, t1=t1, cg=cg: e.tensor_tensor(out=t1, in0=cg, in1=cg, op=ALU.mult),
                     reads=[cg_r], writes=[t1_r])
                S.op("dve", lambda e, t1=t1: e.tensor_scalar(out=t1, in0=t1, scalar1=0.044715, scalar2=1.0,
                                                              op0=ALU.mult, op1=ALU.add),
                     reads=[t1_r], writes=[t1_r])
                S.op("dve", lambda e, t1=t1, cg=cg: e.tensor_tensor(out=t1, in0=t1, in1=cg, op=ALU.mult),
                     reads=[t1_r, cg_r], writes=[t1_r])
                S.op("act", lambda e, t1=t1, t2=t2: e.activation(out=t2, in_=t1, func=AF.Sigmoid, scale=GELU_C),
                     reads=[t1_r], writes=[t2_r])
                S.op("dve", lambda e, t2=t2, cg=cg: e.tensor_tensor(out=t2, in0=t2, in1=cg, op=ALU.mult),
                     reads=[t2_r, cg_r], writes=[t2_r])
                S.op("dve", lambda e, t2=t2, cv=cv, f=f: e.tensor_tensor(out=ACTs[:, f, :], in0=t2, in1=cv,
                                                                          op=ALU.mult),
                     reads=[t2_r, cv_r], writes=[ACTs_r])
            S.barrier()
            A.off = d1_mark
            wdn = [(A.alloc(11 * 512 * 2, BF16).rearrange("p (f c) -> p f c", f=11), Res("wdn%d" % s))
                   for s in range(3)]
            fst = [(A.alloc(512 * 4, F32), Res("fst%d" % s)) for s in range(3)]
            jk = [(A.alloc(512 * 2, BF16), Res("jk%d" % s)) for s in range(2)]
            if i == 0:
                ssq2 = consts["ssq2"]
                S.op("dve", lambda e: e.memset(ssq2[0].rearrange("p a b -> p (a b)"), 0.0), writes=[ssq2[1]])
            nd_ = 0
            ne = 0
            for j in range(8):
                pb0 = 4 * (j % 2)
                for pi_, (f0, nf) in enumerate(pieces):
                    w, w_r = wdn[nd_ % 3]
                    nd_ += 1
                    S.dma("sp", w[:, 0:nf, :], gdn_c[f0 // 43][j][:, f0 % 43:f0 % 43 + nf, :], w_r, writes=[w_r])
                    for sub in range(4):
                        bi = pb0 + sub
                        mm_group(banks[bi][:, :], bank_res[bi],
                                 [(ACTs[:, f0 + q_, sub * 128:(sub + 1) * 128], w[:, q_, :]) for q_ in range(nf)],
                                 [ACTs_r, w_r], start=(pi_ == 0), stop=(pi_ == len(pieces) - 1))
                for sub in range(4):
                    bi = pb0 + sub
                    ps = banks[bi][:, :]
                    st, st_r = fst[ne % 3]
                    jj, jj_r = jk[ne % 2]
                    ne += 1
                    tb_ = i * 4 + sub
                    S.op("act", lambda e, st=st, ps=ps: e.copy(out=st, in_=ps), reads=[bank_res[bi]],
                         writes=[st_r])
                    S.op("act", lambda e, jj=jj, st=st, tb_=tb_, j=j: e.activation(
                        out=jj, in_=st, func=AF.Square, accum_out=ssq2[0][:, tb_, j:j + 1]),
                        reads=[st_r], writes=[jj_r, ssq2[1]])
                    S.dma("pool", FFs[tb_ * 128:(tb_ + 1) * 128, j * 512:(j + 1) * 512], st, st_r, reads=[st_r])
            S.barrier()

        done("D")
        resid_phase(ssq2[0], ssq2[1], "g_post_ffn", FFs, X1, 128, out_d, 0, const_mark)
        S.barrier()


    try:
        body()
    except _Stop:
        if "cc" in state:
            for e_ in S.ENG:
                S._wait(e_, state["cc"])
        S.barrier()

    with nc.Block() as block:
        @block.tensor
        def _(e):
            for f in S.prog["pe"]:
                f(e)

        @block.scalar
        def _(e):
            for f in S.prog["act"]:
                f(e)

        @block.vector
        def _(e):
            for f in S.prog["dve"]:
                f(e)

        @block.gpsimd
        def _(e):
            for f in S.prog["pool"]:
                f(e)

        @block.sync
        def _(e):
            for f in S.prog["sp"]:
                f(e)
    return nc, es


def _consts():
    ident = np.eye(128, dtype=np.float32)
    j = np.arange(128)[:, None]
    s = np.arange(128)[None, :]
    tri = np.where(j >= s, -1.0, 0.0).astype(np.float32)
    nones = -np.ones((128, 128), np.float32)
    t = np.arange(512)[None, :]
    masks = np.concatenate([(128 * r + j < t).astype(np.float32) for r in range(4)], axis=1)
    return ident, tri, nones, masks


def kernel(x, pre_mix_norm, w_in, pool_w, pool_scale, w_out, post_mix_norm, pre_ffn_norm,
           w_up, conv_w, conv_b, w_down, post_ffn_norm, _stop=None, _trace=False):
    x = np.asarray(x, np.float32)
    w_in = np.asarray(w_in, np.float32)[0]
    w_out = np.asarray(w_out, np.float32)[0]
    w_up = np.asarray(w_up, np.float32)[0]
    w_down = np.asarray(w_down, np.float32)[0]
    pool_w = np.asarray(pool_w, np.float32)[0]
    ident, tri, nones, masks = _consts()
    pscale = np.ascontiguousarray(np.asarray(pool_scale, np.float32)[0].reshape(16, 128).T)
    cw = np.ascontiguousarray(
        np.asarray(conv_w, np.float32)[0].reshape(3, 172, 128).transpose(2, 0, 1)).reshape(128, 3 * 172)
    cb = np.ascontiguousarray(np.asarray(conv_b, np.float32)[0].reshape(172, 128).T)
    gains = {
        "g_pre_mix": np.asarray(pre_mix_norm, np.float32).reshape(1, D),
        "g_post_mix": np.asarray(post_mix_norm, np.float32).reshape(1, D),
        "g_pre_ffn": np.asarray(pre_ffn_norm, np.float32).reshape(1, D),
        "g_post_ffn": np.asarray(post_ffn_norm, np.float32).reshape(1, D),
    }
    wu = np.ascontiguousarray(w_in[:, 6144:8192])
    poolw = np.ascontiguousarray(pool_w.reshape(2048, 512))
    in_maps = []
    for c in range(NCORES):
        b, p = c // 4, c % 4
        t0 = TOWN * p
        halo = x[b, t0 - 128:t0] if p > 0 else np.zeros((128, D), np.float32)
        xo = np.concatenate([halo, x[b, t0:t0 + TOWN]], axis=0)
        cols = slice(512 * p, 512 * p + 512)
        wqkv = np.concatenate([w_in[:, 0:2048][:, cols], w_in[:, 2048:4096][:, cols],
                               w_in[:, 4096:6144][:, cols]], axis=1)
        ic = np.zeros((4, 16), np.float32)
        for g, w in enumerate(POOL_W):
            pos = t0 + np.arange(16) + 1
            ic[g] = 1.0 / np.minimum(pos, w)
        ic = np.broadcast_to(ic.reshape(1, 64), (128, 64))
        sela = np.zeros((128, 8), np.float32)
        sela[:, b * 4 + p] = 1.0
        selh = np.zeros((128, 8), np.float32)
        if p > 0:
            selh[:, c - 1] = 1.0
        m = {
            "xb": np.ascontiguousarray(x[b]),
            "xo": np.ascontiguousarray(xo),
            "wqkv": np.ascontiguousarray(wqkv),
            "wu": wu,
            "poolw": poolw,
            "wout_s": np.ascontiguousarray(w_out[512 * c:512 * c + 512]),
            "wup_s": np.ascontiguousarray(w_up[512 * c:512 * c + 512]),
            "wdn_s": np.ascontiguousarray(w_down[:, 512 * c:512 * c + 512]),
            "pscale": pscale, "cw": cw, "cb": cb,
            "ident": ident, "tri": tri, "nones": nones, "masks": masks,
            "ic": np.ascontiguousarray(ic), "sela": sela, "selh": selh,
        }
        m.update(gains)
        in_maps.append(m)
    nc, es = build_program(_stop)
    with es:
        if _trace:
            res = run_bass_kernel_spmd(nc, in_maps, core_ids=list(range(NCORES)), trace=True)
            print("exec_time_ns", res.exec_time_ns)
        else:
            res = run_bass_kernel_spmd(nc, in_maps, core_ids=list(range(NCORES)))
    out = np.empty((2, SEQ, D), np.float32)
    for c in range(NCORES):
        b, p = c // 4, c % 4
        out[b, TOWN * p:TOWN * (p + 1)] = res.results[c]["out"]
    return out
```
